# Optimizing a Trainium2 kernel written in Bass

```python
import jax, jax.numpy as jnp
from jax import lax
import numpy as np

D_MODEL = 2048
BATCH = 2
SEQ = 4096
DEPTH = 4

N_MIXERS = 2
EPS = 1e-6

HGRN_EXPAND = 128
HGRN_HEADS = D_MODEL // HGRN_EXPAND
HGRN_DK = HGRN_EXPAND
HGRN_DV = D_MODEL // HGRN_HEADS
HGRN_CHUNK = 64
HGRN_IN = 5 * D_MODEL

HEAD_DIM = 128
ATTN_HEADS = D_MODEL // HEAD_DIM
DILATED_GROUPS = ((128, 1), (512, 4), (2048, 16))
N_GROUPS = len(DILATED_GROUPS)
ATTN_IN = N_GROUPS * 3 * ATTN_HEADS * HEAD_DIM
ROPE_THETA = 500000.0
ROPE_DIM = HEAD_DIM // 4
NEG_INF = -1e30

D_FF = ((8 * D_MODEL // 3 + 255) // 256) * 256

N_HGRN_LAYERS = (DEPTH + 1) // 2
N_ATTN_LAYERS = DEPTH // 2

kernel_name = "hybrid_hgrn2_dilated_attn_encoder"


def rmsnorm(x, w):
    x32 = x.astype(jnp.float32)
    y = x32 * lax.rsqrt(jnp.mean(x32 * x32, axis=-1, keepdims=True) + EPS)
    return (y * w.astype(jnp.float32)).astype(x.dtype)


def chunk_gla(q, k, v, log_f):
    B, H, S, dk = q.shape
    dv = v.shape[-1]
    n = S // HGRN_CHUNK

    def to_chunks(t):
        return jnp.moveaxis(t.reshape(B, H, n, HGRN_CHUNK, t.shape[-1]), 2, 0)

    tri = jnp.tril(jnp.ones((HGRN_CHUNK, HGRN_CHUNK), dtype=bool))

    def step(state, inp):
        qc, kc, vc, gc = inp
        b = jnp.cumsum(gc, axis=2)
        diff = b[:, :, :, None, :] - b[:, :, None, :, :]
        decay = jnp.exp(jnp.where(tri[:, :, None], diff, -jnp.inf))
        scores = jnp.einsum('bhtd,bhsd,bhtsd->bhts', qc, kc, decay)
        o = (jnp.einsum('bhts,bhsv->bhtv', scores, vc)
             + jnp.einsum('bhtd,bhdv->bhtv', qc * jnp.exp(b), state))
        b_last = b[:, :, -1:, :]
        state = (jnp.exp(b_last[:, :, 0, :, None]) * state
                 + jnp.einsum('bhsd,bhsv->bhdv', kc * jnp.exp(b_last - b), vc))
        return state, o

    state0 = jnp.zeros((B, H, dk, dv), jnp.float32)
    _, o = lax.scan(step, state0, (to_chunks(q), to_chunks(k), to_chunks(v), to_chunks(log_f)))
    return jnp.moveaxis(o, 0, 2).reshape(B, H, S, dv)


def hgrn2_mixer(h, w_in, lb_fwd, lb_bwd, g_norm_w):
    B, S, _ = h.shape
    proj = jnp.einsum('bsd,de->bse', h, w_in)
    q, f_fwd, f_bwd, i, g = jnp.split(proj, 5, axis=-1)

    def heads(t, dh):
        return t.reshape(B, S, HGRN_HEADS, dh).transpose(0, 2, 1, 3).astype(jnp.float32)

    q = heads(jax.nn.silu(q), HGRN_DK)
    v = heads(i, HGRN_DV)

    def gates(f, lb):
        fg = lb + (1.0 - lb) * jax.nn.sigmoid(f.astype(jnp.float32))
        return heads(1.0 - fg, HGRN_DK), heads(jnp.log(fg), HGRN_DK)

    k_f, lg_f = gates(f_fwd, lb_fwd)
    k_b, lg_b = gates(f_bwd, lb_bwd)
    o_f = chunk_gla(q, k_f, v, lg_f)
    o_b = jnp.flip(chunk_gla(jnp.flip(q, 2), jnp.flip(k_b, 2), jnp.flip(v, 2), jnp.flip(lg_b, 2)), 2)
    o = rmsnorm(o_f + o_b, g_norm_w)
    o = o.transpose(0, 2, 1, 3).reshape(B, S, D_MODEL).astype(h.dtype)
    return o * jax.nn.silu(g)


def partial_rope(x, cos, sin):
    xr, xp = x[..., :ROPE_DIM], x[..., ROPE_DIM:]
    x1, x2 = jnp.split(xr, 2, axis=-1)
    rot = jnp.concatenate([-x2, x1], axis=-1)
    xr = (xr * cos + rot * sin).astype(x.dtype)
    return jnp.concatenate([xr, xp], axis=-1)


def dilated_group_attention(q, k, v, window, dilation):
    B, S, H, dh = q.shape
    radius = window // (2 * dilation)
    W = radius
    L = S // dilation
    nblk = -(-L // W)
    Lp = nblk * W

    def strided(t, pad_lo, pad_hi):
        t = t.reshape(B, L, dilation, H, dh)
        return jnp.pad(t, ((0, 0), (pad_lo, pad_hi), (0, 0), (0, 0), (0, 0)))

    qs = strided(q, 0, Lp - L).reshape(B, nblk, W, dilation, H, dh).astype(jnp.float32)

    def neighbourhood(t):
        t = strided(t, W, Lp - L + W).reshape(B, nblk + 2, W, dilation, H, dh)
        return jnp.concatenate([t[:, :-2], t[:, 1:-1], t[:, 2:]], axis=2).astype(jnp.float32)

    kn = neighbourhood(k)
    vn = neighbourhood(v)
    u_q = jnp.arange(nblk)[:, None] * W + jnp.arange(W)[None, :]
    u_k = (jnp.arange(nblk)[:, None] - 1) * W + jnp.arange(3 * W)[None, :]
    rel = u_k[:, None, :] - u_q[:, :, None]
    mask = (jnp.abs(rel) <= radius) & (u_k[:, None, :] >= 0) & (u_k[:, None, :] < L)

    s = jnp.einsum('bnidhe,bnjdhe->bndhij', qs, kn) * (HEAD_DIM ** -0.5)
    s = jnp.where(mask[None, :, None, None], s, NEG_INF)
    m = jnp.max(s, axis=-1, keepdims=True)
    p = jnp.exp(s - m)
    l = jnp.sum(p, axis=-1)
    o = jnp.einsum('bndhij,bnjdhe->bnidhe', p, vn)
    l_t = l.transpose(0, 1, 4, 2, 3)
    o = o / l_t[..., None]
    lse = (m[..., 0] + jnp.log(l)).transpose(0, 1, 4, 2, 3)
    o = o.reshape(B, Lp, dilation, H, dh)[:, :L].reshape(B, S, H, dh)
    lse = lse.reshape(B, Lp, dilation, H)[:, :L].reshape(B, S, H)
    return o, lse


def dilated_mixer(h, w_in, cos, sin):
    B, S, _ = h.shape
    proj = jnp.einsum('bsd,de->bse', h, w_in).reshape(B, S, N_GROUPS, 3, ATTN_HEADS, HEAD_DIM)
    outs, lses = [], []
    for g, (window, dilation) in enumerate(DILATED_GROUPS):
        q = partial_rope(proj[:, :, g, 0], cos, sin)
        k = partial_rope(proj[:, :, g, 1], cos, sin)
        v = proj[:, :, g, 2]
        o, lse = dilated_group_attention(q, k, v, window, dilation)
        outs.append(o)
        lses.append(lse)
    wts = jax.nn.softmax(jnp.stack(lses, axis=0), axis=0)
    o = jnp.sum(wts[..., None] * jnp.stack(outs, axis=0), axis=0)
    return o.reshape(B, S, ATTN_HEADS * HEAD_DIM).astype(h.dtype)


def swiglu_ffn(h, w_in, w_out):
    gate, up = jnp.split(jnp.einsum('bsd,df->bsf', h, w_in), 2, axis=-1)
    return jnp.einsum('bsf,fd->bsd', jax.nn.silu(gate) * up, w_out)


def setup_inputs(seed: int = 0) -> dict:
    key = jax.random.key(seed)
    ks = jax.random.split(key, 9)
    f32 = jnp.float32
    x = jax.random.normal(ks[0], (BATCH, SEQ, D_MODEL), f32)
    norm_w = 1.0 + 0.02 * jax.random.normal(ks[1], (DEPTH, 4, D_MODEL), f32)
    w_in_hgrn = jax.random.normal(ks[2], (N_HGRN_LAYERS, D_MODEL, HGRN_IN), f32) * D_MODEL ** -0.5
    hgrn_lower_bounds = 0.1 * jax.random.normal(ks[3], (2, DEPTH, HGRN_HEADS * HGRN_DK), f32)
    hgrn_gnorm = 1.0 + 0.02 * jax.random.normal(ks[4], (N_HGRN_LAYERS, HGRN_DV), f32)
    w_in_attn = jax.random.normal(ks[5], (N_ATTN_LAYERS, D_MODEL, ATTN_IN), f32) * D_MODEL ** -0.5
    w_out = jax.random.normal(ks[6], (DEPTH, D_MODEL, D_MODEL), f32) * D_MODEL ** -0.5
    w_ffn_in = jax.random.normal(ks[7], (DEPTH, D_MODEL, 2 * D_FF), f32) * D_MODEL ** -0.5
    w_ffn_out = jax.random.normal(ks[8], (DEPTH, D_FF, D_MODEL), f32) * D_FF ** -0.5
    return {"x": x, "norm_w": norm_w, "w_in_hgrn": w_in_hgrn,
            "hgrn_lower_bounds": hgrn_lower_bounds, "hgrn_gnorm": hgrn_gnorm,
            "w_in_attn": w_in_attn, "w_out": w_out,
            "w_ffn_in": w_ffn_in, "w_ffn_out": w_ffn_out}


def reference(x, norm_w, w_in_hgrn, hgrn_lower_bounds, hgrn_gnorm, w_in_attn, w_out, w_ffn_in, w_ffn_out):
    S = x.shape[1]
    lb = jnp.cumsum(jax.nn.softmax(hgrn_lower_bounds.astype(jnp.float32), axis=1), axis=1)
    lb = lb - lb[:, :1]
    pos = jnp.arange(S, dtype=jnp.float32)
    inv_freq = ROPE_THETA ** (-(jnp.arange(0, ROPE_DIM, 2, dtype=jnp.float32) / ROPE_DIM))
    ang = pos[:, None] * inv_freq[None, :]
    ang = jnp.concatenate([ang, ang], axis=-1)
    cos = jnp.cos(ang)[:, None, :]
    sin = jnp.sin(ang)[:, None, :]

    for layer in range(DEPTH):
        mixer = layer % N_MIXERS
        slot = layer // N_MIXERS
        h = rmsnorm(x, norm_w[layer, 0])
        if mixer == 0:
            h = hgrn2_mixer(h, w_in_hgrn[slot], lb[0, layer], lb[1, layer], hgrn_gnorm[slot])
        else:
            h = dilated_mixer(h, w_in_attn[slot], cos, sin)
        h = jnp.einsum('bsd,de->bse', h, w_out[layer])
        x = x + rmsnorm(h, norm_w[layer, 1])
        h = swiglu_ffn(rmsnorm(x, norm_w[layer, 2]), w_ffn_in[layer], w_ffn_out[layer])
        x = x + rmsnorm(h, norm_w[layer, 3])
    return x
```

```python
import numpy as np
import ml_dtypes
import concourse.bass as bass
import concourse.mybir as mybir
from concourse.bass_utils import run_bass_kernel_spmd

F32 = mybir.dt.float32
BF16 = mybir.dt.bfloat16
I32 = mybir.dt.int32
AF = mybir.ActivationFunctionType
ALU = mybir.AluOpType

NCORES = 8
D = 2048
KC = 16
SEQ = 4096
NTOK = 8192
TPC = 1024
DFF = 5632
FC = 44
DEPTH = 4
EPS = 1e-6
import os
DBG = int(os.environ.get('DBG', '9'))


class Sched:
    ENGS = ("pe", "dve", "act", "pool", "sp")

    def __init__(self, nc, n_slots=8):
        self.nc = nc
        self.ops = []
        self.n_slots = n_slots

    def op(self, eng, fn, reads=(), writes=(), dma=False):
        rd = tuple(r for r in reads if not (isinstance(r, tuple) and r[0] == "P"))
        wr = tuple(writes) + tuple(r for r in reads if isinstance(r, tuple) and r[0] == "P")
        self.ops.append((eng, fn, rd, wr, dma))
        return len(self.ops) - 1

    def pe(self, fn, reads=(), writes=()):
        return self.op("pe", fn, reads, writes)

    def dve(self, fn, reads=(), writes=()):
        return self.op("dve", fn, reads, writes)

    def act(self, fn, reads=(), writes=()):
        return self.op("act", fn, reads, writes)

    def pool(self, fn, reads=(), writes=()):
        return self.op("pool", fn, reads, writes)

    def dma(self, eng, fn, reads=(), writes=()):
        return self.op(eng, fn, reads, writes, dma=True)

    def emit(self, final_wait_ops=()):
        nc = self.nc
        ops = self.ops
        n = len(ops)
        last_w = {}
        readers = {}
        deps = [None] * n
        for i, (eng, fn, rd, wr, dma) in enumerate(ops):
            d = set()
            for r in rd:
                if r in last_w:
                    d.add(last_w[r])
            for w in wr:
                if w in last_w:
                    d.add(last_w[w])
                for j in readers.get(w, ()):
                    d.add(j)
            d.discard(i)
            deps[i] = d
            for r in rd:
                readers.setdefault(r, []).append(i)
            for w in wr:
                last_w[w] = i
                readers[w] = []
        csem = {e: nc.alloc_semaphore(f"c_{e}") for e in self.ENGS}
        dsem = {e: [nc.alloc_semaphore(f"d_{e}_{k}") for k in range(self.n_slots)]
                for e in ("sp", "act", "pool")}
        ccount = {e: 0 for e in self.ENGS}
        dcount = {e: 0 for e in ("sp", "act", "pool")}
        event = [None] * n
        slot_wait = [None] * n
        for i, (eng, fn, rd, wr, dma) in enumerate(ops):
            if dma:
                k = dcount[eng]
                dcount[eng] += 1
                s = dsem[eng][k % self.n_slots]
                event[i] = (s, 16 * (k // self.n_slots + 1), "d", eng)
                if k >= self.n_slots:
                    slot_wait[i] = (s, 16 * (k // self.n_slots))
            else:
                ccount[eng] += 1
                event[i] = (csem[eng], ccount[eng], "c", eng)
        per_eng = {e: [] for e in self.ENGS}
        for i, o in enumerate(ops):
            per_eng[o[0]].append(i)
        final_waits = [event[i] for i in final_wait_ops]

        def run(engname, engobj):
            known = {}
            for i in per_eng[engname]:
                eng, fn, rd, wr, dma = ops[i]
                waits = {}
                for j in deps[i]:
                    s, v, kind, ej = event[j]
                    if kind == "c" and ej == engname and engname == "pe":
                        continue
                    key = id(s)
                    if known.get(key, 0) >= v:
                        continue
                    if key not in waits or waits[key][1] < v:
                        waits[key] = (s, v)
                if slot_wait[i] is not None:
                    s, v = slot_wait[i]
                    key = id(s)
                    if known.get(key, 0) < v and (key not in waits or waits[key][1] < v):
                        waits[key] = (s, v)
                for key, (s, v) in waits.items():
                    engobj.wait_ge(s, v)
                    known[key] = v
                ins = fn(engobj)
                s, v, kind, _ = event[i]
                ins.then_inc(s, 16 if kind == "d" else 1)
            if engname == "sp":
                for (s, v, kind, _) in final_waits:
                    if known.get(id(s), 0) < v:
                        engobj.wait_ge(s, v)
                        known[id(s)] = v

        with nc.Block() as block:
            @block.tensor
            def _(e):
                run("pe", e)

            @block.vector
            def _(e):
                run("dve", e)

            @block.scalar
            def _(e):
                run("act", e)

            @block.gpsimd
            def _(e):
                run("pool", e)

            @block.sync
            def _(e):
                run("sp", e)


class WStream:
    def __init__(self, S, nc, name, slot_cols, nslots):
        self.S = S
        self.name = name
        self.nslots = nslots
        self.slots = [nc.alloc_sbuf_tensor(f"{name}_s{i}", [128, slot_cols], BF16)
                      for i in range(nslots)]
        self.items = []
        self.loaded = 0
        self.used = 0

    def plan(self, aps):
        self.items.extend(aps)

    def _issue(self, i):
        ap = self.items[i]
        ncols = ap.shape[-1]
        sl = self.slots[i % self.nslots]
        key = (self.name, i % self.nslots)
        self.S.dma("pool", lambda e, sl=sl, ap=ap, ncols=ncols: e.dma_start(out=sl[:, 0:ncols], in_=ap),
                   writes=[key])

    def prefetch(self, upto):
        upto = min(upto, len(self.items))
        while self.loaded < upto:
            self._issue(self.loaded)
            self.loaded += 1

    def next(self):
        i = self.used
        self.prefetch(i + self.nslots - 1)
        self.used += 1
        return self.slots[i % self.nslots], (self.name, i % self.nslots)


def w_tile(W, c0, r0=0, nk=None):
    if nk is None:
        nk = (W.shape[0] - r0) // 128
    blk = W[r0:r0 + nk * 128, c0:c0 + 128].reshape(nk, 128, 128)
    return np.ascontiguousarray(blk.transpose(1, 0, 2)).reshape(128, nk * 128)


C_COLS = 16 * 2048 + 88 * 2048 + 32 * 2816


def pack_phaseC_weights(w_out_l, w_ffn_in_l, w_ffn_out_l):
    out = np.empty((128, C_COLS), np.float32)
    off = 0
    for m in range(16):
        out[:, off:off + 2048] = w_tile(w_out_l, m * 128)
        off += 2048
    for dh in range(2):
        for jj in range(22):
            j = dh * 22 + jj
            out[:, off:off + 2048] = w_tile(w_ffn_in_l, j * 128)
            off += 2048
            out[:, off:off + 2048] = w_tile(w_ffn_in_l, DFF + j * 128)
            off += 2048
        for m in range(16):
            out[:, off:off + 2816] = w_tile(w_ffn_out_l, m * 128, r0=dh * 2816, nk=22)
            off += 2816
    assert off == C_COLS
    return out


def phaseC_items(wc_ap):
    items = []
    off = 0
    for m in range(16):
        items.append(wc_ap[:, off:off + 2048])
        off += 2048
    for dh in range(2):
        for jj in range(22):
            items.append(wc_ap[:, off:off + 2048])
            off += 2048
            items.append(wc_ap[:, off:off + 2048])
            off += 2048
        for m in range(16):
            items.append(wc_ap[:, off:off + 2816])
            off += 2816
    return items


class CTiles:
    def __init__(self, nc):
        self.oT = nc.alloc_sbuf_tensor("c_oT", [128, KC, 512], BF16)
        self.hT = nc.alloc_sbuf_tensor("c_hT", [128, KC, 512], BF16)
        self.ybuf = nc.alloc_sbuf_tensor("c_ybuf", [128, KC, 512], F32)
        self.act = nc.alloc_sbuf_tensor("c_act", [128, 22, 512], BF16)
        self.sq = [nc.alloc_sbuf_tensor(f"c_sq{i}", [128, 512], BF16) for i in range(2)]
        self.rs = nc.alloc_sbuf_tensor("c_rs", [128, 512], F32)
        self.tmp = [nc.alloc_sbuf_tensor(f"c_tmp{i}", [128, 512], F32) for i in range(2)]
        self.sg = [nc.alloc_sbuf_tensor(f"c_sg{i}", [128, 512], F32) for i in range(2)]
        self.yps = [nc.alloc_psum_tensor(f"c_yps{i}", [128, 512], F32) for i in range(2)]
        self.ssps = nc.alloc_psum_tensor("c_ssps", [128, 512], F32)
        self.gps = [nc.alloc_psum_tensor(f"c_gps{i}", [128, 512], F32) for i in range(2)]
        self.ups = [nc.alloc_psum_tensor(f"c_ups{i}", [128, 512], F32) for i in range(2)]


def emit_rstd(S, T, ndim):
    S.act(lambda e: e.activation(out=T.rs[:], in_=T.ssps[:], func=AF.Sqrt, bias=EPS, scale=1.0 / ndim),
          reads=[("P", "ss")], writes=["c_rs"])
    S.dve(lambda e: e.reciprocal(T.rs[:], T.rs[:]), reads=["c_rs"], writes=["c_rs"])


def emit_sumsq(S, T, ones, src_fn, src_keys, m, nm):
    sq = T.sq[m % 2]
    S.act(lambda e: e.activation(out=sq[:], in_=src_fn(), func=AF.Square),
          reads=src_keys, writes=[("c_sq", m % 2)])
    S.pe(lambda e: e.matmul(T.ssps[:], ones[:], sq[:], start=(m == 0), stop=(m == nm - 1)),
         reads=[("c_sq", m % 2), "ones"], writes=[("P", "ss")])


def emit_x_update(S, T, xT, nw, l, which, tok):
    for m in range(KC):
        tmp = T.tmp[m % 2]
        S.dve(lambda e, m=m, tmp=tmp: e.tensor_tensor(tmp[:], T.ybuf[:, m, :], T.rs[:], ALU.mult),
              reads=[("c_ybuf", m), "c_rs"], writes=[("c_tmp", m % 2)])
        S.dve(lambda e, m=m, tmp=tmp: e.scalar_tensor_tensor(
            out=xT[:, m, tok], in0=tmp[:], scalar=nw[:, l, which, m:m + 1], in1=xT[:, m, tok],
            op0=ALU.mult, op1=ALU.add),
            reads=[("c_tmp", m % 2), ("xT", m), "nw"], writes=[("xT", m)])


def emit_norm_to_bf16(S, T, ones, xT, nw, l, which, tok):
    for m in range(KC):
        emit_sumsq(S, T, ones, lambda m=m: xT[:, m, tok], [("xT", m)], m, KC)
    emit_rstd(S, T, D)
    for m in range(KC):
        S.dve(lambda e, m=m: e.scalar_tensor_tensor(
            out=T.hT[:, m, :], in0=xT[:, m, tok], scalar=nw[:, l, which, m:m + 1], in1=T.rs[:],
            op0=ALU.mult, op1=ALU.mult),
            reads=[("xT", m), "c_rs", "nw"], writes=[("c_hT", m)])


def emit_phaseC_half(S, T, ws, ones, xT, nw, l, hf, o_dram, h_next_dram, out_dram, stop_after=99):
    tok = slice(hf * 512, hf * 512 + 512)
    S.dma("sp", lambda e: e.dma_start(out=T.oT[:], in_=o_dram[:, :, tok].rearrange("k p t -> p k t")),
          writes=[("c_oT", k) for k in range(KC)])
    for m in range(KC):
        wt, wkey = ws.next()
        yps = T.yps[m % 2]
        for kc in range(KC):
            S.pe(lambda e, kc=kc, wt=wt, yps=yps: e.matmul(
                yps[:], wt[:, kc * 128:(kc + 1) * 128], T.oT[:, kc, :], start=(kc == 0), stop=(kc == KC - 1)),
                reads=[wkey, ("c_oT", kc)], writes=[("P", "yps", m % 2)])
        if DBG >= 1:
            S.dve(lambda e, m=m, yps=yps: e.tensor_copy(T.ybuf[:, m, :], yps[:]),
                  reads=[("P", "yps", m % 2)], writes=[("c_ybuf", m)])
        if DBG >= 2:
            emit_sumsq(S, T, ones, lambda m=m: T.ybuf[:, m, :], [("c_ybuf", m)], m, KC)
    if DBG >= 3:
        emit_rstd(S, T, D)
    if DBG >= 4:
        emit_x_update(S, T, xT, nw, l, 1, tok)
    if stop_after <= 1:
        return [S.dma("sp", lambda e: e.dma_start(out=out_dram[:, :, tok].rearrange("k p t -> p k t"), in_=xT[:, :, tok]),
                  reads=[("xT", k) for k in range(KC)], writes=[("outx", hf)])]
    emit_norm_to_bf16(S, T, ones, xT, nw, l, 2, tok)
    if stop_after == 2:
        o1 = S.dma("sp", lambda e: e.dma_start(out=h_next_dram[:, :, tok].rearrange("k p t -> p k t"), in_=T.hT[:]),
                  reads=[("c_hT", k) for k in range(KC)], writes=[("hx", l + 1, hf)])
        o2 = S.dma("sp", lambda e: e.dma_start(out=out_dram[:, :, tok].rearrange("k p t -> p k t"), in_=xT[:, :, tok]),
                  reads=[("xT", k) for k in range(KC)], writes=[("outx", hf)])
        return [o1, o2]
    for dh in range(2):
        for jj in range(22):
            wg, gkey = ws.next()
            wu, ukey = ws.next()
            gps = T.gps[jj % 2]
            ups = T.ups[jj % 2]
            sg = T.sg[jj % 2]
            for kc in range(KC):
                S.pe(lambda e, kc=kc, wg=wg, gps=gps: e.matmul(
                    gps[:], wg[:, kc * 128:(kc + 1) * 128], T.hT[:, kc, :], start=(kc == 0), stop=(kc == KC - 1)),
                    reads=[gkey, ("c_hT", kc)], writes=[("P", "g", jj % 2)])
            for kc in range(KC):
                S.pe(lambda e, kc=kc, wu=wu, ups=ups: e.matmul(
                    ups[:], wu[:, kc * 128:(kc + 1) * 128], T.hT[:, kc, :], start=(kc == 0), stop=(kc == KC - 1)),
                    reads=[ukey, ("c_hT", kc)], writes=[("P", "u", jj % 2)])
            S.act(lambda e, gps=gps, sg=sg: e.activation(out=sg[:], in_=gps[:], func=AF.Silu),
                  reads=[("P", "g", jj % 2)], writes=[("c_sg", jj % 2)])
            S.dve(lambda e, jj=jj, ups=ups, sg=sg: e.tensor_tensor(T.act[:, jj, :], sg[:], ups[:], ALU.mult),
                  reads=[("c_sg", jj % 2), ("P", "u", jj % 2)], writes=[("c_act", jj)])
        if stop_after == 3:
            o1 = S.dma("sp", lambda e: e.dma_start(out=h_next_dram[:, :, tok].rearrange("k p t -> p k t"), in_=T.act[:, 0:16, :]),
                      reads=[("c_act", k) for k in range(22)], writes=[("hx", l + 1, hf)])
            for _ in range(16 + 44 + 16):
                ws.next()
            return [o1]
        for m in range(KC):
            wt, wkey = ws.next()
            yps = T.yps[m % 2]
            for jj in range(22):
                S.pe(lambda e, jj=jj, wt=wt, yps=yps: e.matmul(
                    yps[:], wt[:, jj * 128:(jj + 1) * 128], T.act[:, jj, :], start=(jj == 0), stop=(jj == 21)),
                    reads=[wkey, ("c_act", jj)], writes=[("P", "yps", m % 2)])
            if dh == 0:
                S.dve(lambda e, m=m, yps=yps: e.tensor_copy(T.ybuf[:, m, :], yps[:]),
                      reads=[("P", "yps", m % 2)], writes=[("c_ybuf", m)])
            else:
                S.dve(lambda e, m=m, yps=yps: e.tensor_tensor(T.ybuf[:, m, :], T.ybuf[:, m, :], yps[:], ALU.add),
                      reads=[("P", "yps", m % 2), ("c_ybuf", m)], writes=[("c_ybuf", m)])
                emit_sumsq(S, T, ones, lambda m=m: T.ybuf[:, m, :], [("c_ybuf", m)], m, KC)
    emit_rstd(S, T, D)
    emit_x_update(S, T, xT, nw, l, 3, tok)
    outs = []
    if h_next_dram is not None:
        emit_norm_to_bf16(S, T, ones, xT, nw, l + 1, 0, tok)
        o = S.dma("sp", lambda e: e.dma_start(out=h_next_dram[:, :, tok].rearrange("k p t -> p k t"), in_=T.hT[:]),
                  reads=[("c_hT", k) for k in range(KC)], writes=[("hx", l + 1, hf)])
        outs.append(o)
    if out_dram is not None:
        o = S.dma("sp", lambda e: e.dma_start(out=out_dram[:, :, tok].rearrange("k p t -> p k t"), in_=xT[:, :, tok]),
                  reads=[("xT", k) for k in range(KC)], writes=[("outx", hf)])
        outs.append(o)
    return outs


def make_consts(nc, S):
    ones = nc.alloc_sbuf_tensor("ones", [128, 128], BF16)
    S.dve(lambda e: e.memset(ones[:], 1.0), writes=["ones"])
    return ones


def load_x_and_nw(nc, S, x_in, nw_in):
    xT = nc.alloc_sbuf_tensor("xT", [128, KC, TPC], F32)
    nw = nc.alloc_sbuf_tensor("nw", [128, DEPTH, 4, KC], F32)
    S.dma("sp", lambda e: e.dma_start(out=nw[:], in_=nw_in), writes=["nw"])
    for q in range(4):
        ks = slice(q * 4, q * 4 + 4)
        S.dma("sp", lambda e, ks=ks: e.dma_start(out=xT[:, ks, :], in_=x_in[ks].rearrange("k p t -> p k t")),
              writes=[("xT", k) for k in range(q * 4, q * 4 + 4)])
    return xT, nw


def build_progA():
    nc = bass.Bass("TRN2", target_bir_lowering=False)
    x_in = nc.dram_tensor("x_in", [KC, 128, TPC], F32, kind="ExternalInput").ap()
    nw_in = nc.dram_tensor("nw_in", [128, DEPTH, 4, KC], F32, kind="ExternalInput").ap()
    h_out = nc.dram_tensor("h_out", [KC, 128, TPC], BF16, kind="ExternalOutput").ap()
    S = Sched(nc)
    ones = make_consts(nc, S)
    T = CTiles(nc)
    xT, nw = load_x_and_nw(nc, S, x_in, nw_in)
    outs = []
    for hf in range(2):
        tok = slice(hf * 512, hf * 512 + 512)
        emit_norm_to_bf16(S, T, ones, xT, nw, 0, 0, tok)
        o = S.dma("sp", lambda e, tok=tok: e.dma_start(out=h_out[:, :, tok].rearrange("k p t -> p k t"), in_=T.hT[:]),
                  reads=[("c_hT", k) for k in range(KC)], writes=[("hx", 0, hf)])
        outs.append(o)
    S.emit(final_wait_ops=outs)
    return nc


def build_progC(l, stop_after=99):
    nc = bass.Bass("TRN2", target_bir_lowering=False)
    x_in = nc.dram_tensor("x_in", [KC, 128, TPC], F32, kind="ExternalInput").ap()
    o_in = nc.dram_tensor("o_in", [KC, 128, TPC], BF16, kind="ExternalInput").ap()
    nw_in = nc.dram_tensor("nw_in", [128, DEPTH, 4, KC], F32, kind="ExternalInput").ap()
    wc = nc.dram_tensor("wc", [128, C_COLS if stop_after > 2 else 32768], F32, kind="ExternalInput").ap()
    x_out = nc.dram_tensor("x_out", [KC, 128, TPC], F32, kind="ExternalOutput").ap()
    last = (l == DEPTH - 1)
    h_out = None if last else nc.dram_tensor("h_out", [KC, 128, TPC], BF16, kind="ExternalOutput").ap()
    S = Sched(nc)
    ones = make_consts(nc, S)
    T = CTiles(nc)
    ws = WStream(S, nc, "ws", 2816, 6)
    ws.plan((phaseC_items(wc) if stop_after > 2 else [wc[:, m * 2048:(m + 1) * 2048] for m in range(16)]) * 2)
    xT, nw = load_x_and_nw(nc, S, x_in, nw_in)
    outs = []
    for hf in range(2):
        outs += emit_phaseC_half(S, T, ws, ones, xT, nw, l, hf, o_in, h_out, x_out, stop_after)
    S.emit(final_wait_ops=outs)
    return nc


def hgrn_consts_host():
    s = np.arange(128)[:, None]
    t = np.arange(128)[None, :]
    cm = np.zeros((128, 2, 4, 128), np.int32)
    cm[:, 0] = (s <= t).astype(np.int32)[:, None, :]
    cm[:, 1] = (s >= t).astype(np.int32)[:, None, :]
    rm = np.ones((128, 512), np.float32)
    rm[:, 0::128] = 0.0
    ident = np.eye(128, dtype=np.float32)
    return {"cm": cm, "rm": rm, "ident": ident}


class HTiles:
    def __init__(self, nc, nt):
        seq = nt * 512
        a = nc.alloc_sbuf_tensor
        self.whd = a("h_whd", [128, 5, 2048], BF16)
        self.hTt = [a(f"h_hTt{i}", [128, KC, 512], BF16) for i in range(2)]
        self.qs = a("h_qs", [128, seq], BF16)
        self.Vt = a("h_Vt", [128, nt * 4, 128], BF16)
        self.sgt = a("h_sgt", [128, seq], BF16)
        self.of = a("h_of", [128, seq], F32)
        self.sig = a("h_sig", [128, 512], F32)
        self.sn = a("h_sn", [128, 512], F32)
        self.g = a("h_g", [128, 512], F32)
        self.b = a("h_b", [128, 512], F32)
        self.bex = a("h_bex", [128, 512], F32)
        self.Dd = a("h_Dd", [128, 512], F32)
        self.Eq = a("h_Eq", [128, 512], F32)
        self.Ek = a("h_Ek", [128, 512], F32)
        self.qt = a("h_qt", [128, 512], BF16)
        self.kt = a("h_kt", [128, 512], BF16)
        self.ktok = a("h_ktok", [128, 4, 128], BF16)
        self.At = a("h_At", [128, 4, 128], BF16)
        self.ev = a("h_ev", [128, 4, 4], F32)
        self.S = a("h_S", [128, 128], F32)
        self.Sb = a("h_Sb", [128, 128], BF16)
        self.Pb = a("h_Pb", [128, 128], F32)
        self.osum = a("h_osum", [128, 512], F32)
        self.sq = a("h_sq", [128, 512], BF16)
        self.rs = a("h_rs", [128, 512], F32)
        self.obuf = a("h_obuf", [128, 512], BF16)
        self.cm = a("h_cm", [128, 2, 4, 128], I32)
        self.rm = a("h_rm", [128, 512], F32)
        self.identf = a("h_identf", [128, 128], F32)
        self.ident = a("h_ident", [128, 128], BF16)
        self.hlb = a("h_hlb", [128, 2, 4, 2], F32)
        self.ehlb = a("h_ehlb", [128, 2, 4, 2], F32)
        self.lbv = a("h_lbv", [128, 2, 2], F32)
        self.oml = a("h_oml", [128, 2, 2], F32)
        self.den = a("h_den", [128, 2, 2], F32)
        self.gn = a("h_gn", [128, 1], F32)
        p = nc.alloc_psum_tensor
        self.pj = [p(f"h_pj{i}", [128, 512], F32) for i in range(2)]
        self.pv = p("h_pv", [128, 4, 128], F32)
        self.ptr = p("h_ptr", [128, 4, 128], BF16)
        self.pA = p("h_pA", [128, 4, 128], F32)
        self.pP = p("h_pP", [128, 4, 128], F32)
        self.po = p("h_po", [128, 512], F32)
        self.pss = p("h_pss", [128, 512], F32)


def emit_hgrn_setup(S, H, layer, cm_in, rm_in, ident_in, hlb_in, gn_in):
    S.dma("sp", lambda e: e.dma_start(out=H.cm[:], in_=cm_in), writes=["h_cm"])
    S.dma("sp", lambda e: e.dma_start(out=H.rm[:], in_=rm_in), writes=["h_rm"])
    S.dma("sp", lambda e: e.dma_start(out=H.identf[:], in_=ident_in), writes=["h_identf"])
    S.dma("sp", lambda e: e.dma_start(out=H.hlb[:], in_=hlb_in), writes=["h_hlb"])
    S.dma("sp", lambda e: e.dma_start(out=H.gn[:], in_=gn_in), writes=["h_gn"])
    S.dve(lambda e: e.tensor_copy(H.ident[:], H.identf[:]), reads=["h_identf"], writes=["h_ident"])
    S.act(lambda e: e.activation(out=H.ehlb[:], in_=H.hlb[:], func=AF.Exp), reads=["h_hlb"], writes=["h_ehlb"])
    S.dve(lambda e: e.tensor_tensor(H.den[:], H.ehlb[:, :, 0, :], H.ehlb[:, :, 1, :], ALU.add),
          reads=["h_ehlb"], writes=["h_den"])
    for j in (2, 3):
        S.dve(lambda e, j=j: e.tensor_tensor(H.den[:], H.den[:], H.ehlb[:, :, j, :], ALU.add),
              reads=["h_ehlb", "h_den"], writes=["h_den"])
    S.dve(lambda e: e.reciprocal(H.den[:], H.den[:]), reads=["h_den"], writes=["h_den"])
    S.dve(lambda e: e.memset(H.lbv[:], 0.0), writes=["h_lbv"])
    for j in range(1, layer + 1):
        S.dve(lambda e, j=j: e.tensor_tensor(H.lbv[:], H.lbv[:], H.ehlb[:, :, j, :], ALU.add),
              reads=["h_ehlb", "h_lbv"], writes=["h_lbv"])
    S.dve(lambda e: e.tensor_tensor(H.lbv[:], H.lbv[:], H.den[:], ALU.mult), reads=["h_lbv", "h_den"], writes=["h_lbv"])
    S.dve(lambda e: e.tensor_scalar(H.oml[:], H.lbv[:], -1.0, 1.0, ALU.mult, ALU.add), reads=["h_lbv"], writes=["h_oml"])


def emit_hgrn(S, H, ones, nb, nh, nt, wh_dram, hfull, o_out, dbg=None):
    seq = nt * 512
    outs = []
    jobs = []
    for hh in range(nh):
        for b in range(nb):
            for tt in range(nt):
                jobs.append((hh, b, 0, tt))
            for tt in reversed(range(nt)):
                jobs.append((hh, b, 1, tt))

    def load_h(n):
        hh, b, dr, tt = jobs[n]
        start = b * seq + tt * 512
        src, t0 = start // 1024, start % 1024
        buf = H.hTt[n % 2]
        S.dma("sp", lambda e: e.dma_start(out=buf[:], in_=hfull[src, :, :, t0:t0 + 512].rearrange("k p t -> p k t")),
              writes=[("h_hTt", n % 2)])

    def proj(n, widx, ps, pskey):
        buf = H.hTt[n % 2]
        for kc in range(KC):
            S.pe(lambda e, kc=kc: e.matmul(ps[:], H.whd[:, widx, kc * 128:(kc + 1) * 128], buf[:, kc, :],
                                           start=(kc == 0), stop=(kc == KC - 1)),
                 reads=["h_whd", ("h_hTt", n % 2)], writes=[pskey])

    b3 = lambda t: t[:].rearrange("p (c t) -> p c t", t=128)
    def job(n, hh, b, dr, tt):
        tok = slice(tt * 512, tt * 512 + 512)
        first_of_seq = (tt == 0) if dr == 0 else (tt == nt - 1)
        if dr == 0 and tt == 0 and b == 0:
            for w in range(5):
                S.dma("pool", lambda e, w=w: e.dma_start(
                    out=H.whd[:, w, :], in_=wh_dram[:, (hh * 5 + w) * 2048:(hh * 5 + w + 1) * 2048]),
                    writes=["h_whd"])
        if n + 1 < len(jobs):
            load_h(n + 1)
        lb = H.lbv[:, dr, hh:hh + 1]
        oml = H.oml[:, dr, hh:hh + 1]
        if dr == 0:
            proj(n, 0, H.pj[0], ("P", "pj", 0))
            S.act(lambda e: e.activation(out=H.qs[:, tok], in_=H.pj[0][:], func=AF.Silu),
                  reads=[("P", "pj", 0)], writes=["h_qs"])
        fps = H.pj[1]
        proj(n, 1 + dr, fps, ("P", "pj", 1))
        S.act(lambda e: e.activation(out=H.sig[:], in_=fps[:], func=AF.Sigmoid), reads=[("P", "pj", 1)], writes=["h_sig"])
        S.act(lambda e: e.activation(out=H.sn[:], in_=fps[:], func=AF.Sigmoid, scale=-1.0),
              reads=[("P", "pj", 1)], writes=["h_sn"])
        if dr == 0:
            proj(n, 4, H.pj[0], ("P", "pj", 0))
            S.act(lambda e: e.activation(out=H.sgt[:, tok], in_=H.pj[0][:], func=AF.Silu),
                  reads=[("P", "pj", 0)], writes=["h_sgt"])
            buf = H.hTt[n % 2]
            for sub in range(4):
                for kc in range(KC):
                    S.pe(lambda e, kc=kc, sub=sub: e.matmul(
                        H.pv[:, sub, :], buf[:, kc, sub * 128:(sub + 1) * 128], H.whd[:, 3, kc * 128:(kc + 1) * 128],
                        start=(kc == 0), stop=(kc == KC - 1)),
                        reads=["h_whd", ("h_hTt", n % 2)], writes=[("P", "pv")])
            S.dve(lambda e: e.tensor_copy(H.Vt[:, tt * 4:tt * 4 + 4, :], H.pv[:]), reads=[("P", "pv")], writes=["h_Vt"])
        S.act(lambda e: e.activation(out=H.g[:], in_=H.sig[:], func=AF.Ln, bias=lb, scale=oml),
              reads=["h_sig", "h_lbv", "h_oml"], writes=["h_g"])
        S.dve(lambda e: e.tensor_tensor_scan(H.b[:], H.rm[:], H.g[:], 0.0, ALU.mult, ALU.add),
              reads=["h_g", "h_rm"], writes=["h_b"])
        if dr == 0:
            src = H.b
            mid = 63
        else:
            S.dve(lambda e: e.tensor_tensor(H.bex[:], H.b[:], H.g[:], ALU.subtract), reads=["h_b", "h_g"], writes=["h_bex"])
            src = H.bex
            mid = 64
        S.dve(lambda e: e.tensor_tensor(b3(H.Dd), b3(src), b3(src)[:, :, mid:mid + 1].to_broadcast([128, 4, 128]), ALU.subtract),
              reads=["h_b", "h_bex"], writes=["h_Dd"])
        sq_, sk_ = (1.0, -1.0) if dr == 0 else (-1.0, 1.0)
        S.act(lambda e: e.activation(out=H.Eq[:], in_=H.Dd[:], func=AF.Exp, scale=sq_), reads=["h_Dd"], writes=["h_Eq"])
        S.act(lambda e: e.activation(out=H.Ek[:], in_=H.Dd[:], func=AF.Exp, scale=sk_), reads=["h_Dd"], writes=["h_Ek"])
        S.dve(lambda e: e.tensor_tensor(H.qt[:], H.qs[:, tok], H.Eq[:], ALU.mult), reads=["h_qs", "h_Eq"], writes=["h_qt"])
        S.dve(lambda e: e.scalar_tensor_tensor(out=H.kt[:], in0=H.sn[:], scalar=oml, in1=H.Ek[:], op0=ALU.mult, op1=ALU.mult),
              reads=["h_sn", "h_Ek", "h_oml"], writes=["h_kt"])
        tot = b3(H.b)[:, :, 127]
        if dr == 0:
            R = b3(H.b)[:, :, 63]
            S.act(lambda e: e.activation(out=H.ev[:, 0, :], in_=R, func=AF.Exp), reads=["h_b"], writes=["h_ev0"])
            S.dve(lambda e: e.tensor_tensor(H.ev[:, 3, :], tot, R, ALU.subtract), reads=["h_b"], writes=["h_ev3"])
            S.act(lambda e: e.activation(out=H.ev[:, 1, :], in_=H.ev[:, 3, :], func=AF.Exp), reads=["h_ev3"], writes=["h_ev1"])
        else:
            R = b3(H.bex)[:, :, 64]
            S.act(lambda e: e.activation(out=H.ev[:, 1, :], in_=R, func=AF.Exp), reads=["h_bex"], writes=["h_ev1"])
            S.dve(lambda e: e.tensor_tensor(H.ev[:, 3, :], tot, R, ALU.subtract), reads=["h_b", "h_bex"], writes=["h_ev3"])
            S.act(lambda e: e.activation(out=H.ev[:, 0, :], in_=H.ev[:, 3, :], func=AF.Exp), reads=["h_ev3"], writes=["h_ev0"])
        S.act(lambda e: e.activation(out=H.ev[:, 2, :], in_=tot, func=AF.Exp), reads=["h_b"], writes=["h_ev2"])
        for c in range(4):
            S.pe(lambda e, c=c: e.transpose(H.ptr[:, c, :], H.kt[:, c * 128:(c + 1) * 128], H.ident[:]),
                 reads=["h_kt", "h_ident"], writes=[("P", "ptr")])
        S.dve(lambda e: e.tensor_copy(H.ktok[:], H.ptr[:]), reads=[("P", "ptr")], writes=["h_ktok"])
        for c in range(4):
            S.pe(lambda e, c=c: e.matmul(H.pA[:, c, :], H.kt[:, c * 128:(c + 1) * 128], H.qt[:, c * 128:(c + 1) * 128],
                                         start=True, stop=True),
                 reads=["h_kt", "h_qt"], writes=[("P", "pA")])
        S.dve(lambda e: e.memset(H.At[:], 0.0), writes=["h_At"])
        S.dve(lambda e: e.copy_predicated(H.At[:], H.cm[:, dr], H.pA[:]), reads=[("P", "pA"), "h_cm", "h_At"], writes=["h_At"])
        for c in range(4):
            S.pe(lambda e, c=c: e.matmul(H.pP[:, c, :], H.ktok[:, c, :], H.Vt[:, tt * 4 + c, :], start=True, stop=True),
                 reads=["h_ktok", "h_Vt"], writes=[("P", "pP")])
        corder = range(4) if dr == 0 else reversed(range(4))
        for ci, c in enumerate(corder):
            first = first_of_seq and ci == 0
            if first:
                S.dve(lambda e: e.memset(H.S[:], 0.0), writes=["h_S"])
            else:
                S.act(lambda e, c=c: e.activation(out=H.Sb[:], in_=H.S[:], func=AF.Copy, scale=H.ev[:, 0, c:c + 1]),
                      reads=["h_S", "h_ev0"], writes=["h_Sb"])
            S.pe(lambda e, c=c, first=first: e.matmul(H.po[:, c * 128:(c + 1) * 128], H.Vt[:, tt * 4 + c, :], H.At[:, c, :],
                                                      start=True, stop=first),
                 reads=["h_Vt", "h_At"], writes=[("P", "po")])
            if not first:
                S.pe(lambda e, c=c: e.matmul(H.po[:, c * 128:(c + 1) * 128], H.Sb[:], H.qt[:, c * 128:(c + 1) * 128],
                                             start=False, stop=True),
                     reads=["h_Sb", "h_qt"], writes=[("P", "po")])
            S.act(lambda e, c=c: e.activation(out=H.Pb[:], in_=H.pP[:, c, :], func=AF.Copy, scale=H.ev[:, 1, c:c + 1]),
                  reads=[("P", "pP"), "h_ev1"], writes=["h_Pb"])
            S.dve(lambda e, c=c: e.scalar_tensor_tensor(out=H.S[:], in0=H.S[:], scalar=H.ev[:, 2, c:c + 1], in1=H.Pb[:],
                                                        op0=ALU.mult, op1=ALU.add),
                  reads=["h_S", "h_Pb", "h_ev2"], writes=["h_S"])
        if dbg is not None and n == 0:
            S.dve(lambda e: e.tensor_copy(H.osum[:], H.qt[:]), reads=["h_qt"], writes=["h_osum"])
            S.dve(lambda e: e.tensor_copy(H.rs[:], H.kt[:]), reads=["h_kt"], writes=["h_rs"])
            for i, (tl, key) in enumerate([(H.sig, "h_sig"), (H.g, "h_g"), (H.b, "h_b"), (H.Dd, "h_Dd"), (H.Eq, "h_Eq"),
                                           (H.Ek, "h_Ek"), (H.osum, "h_osum"), (H.rs, "h_rs")]):
                outs.append(S.dma("sp", lambda e, i=i, tl=tl: e.dma_start(out=dbg[:, i, :], in_=tl[:]), reads=[key], writes=[("dbg", i)]))
            S.act(lambda e: e.activation(out=H.osum[:], in_=H.po[:], func=AF.Copy), reads=[("P", "po")], writes=["h_osum"])
            outs.append(S.dma("sp", lambda e: e.dma_start(out=dbg[:, 8, :], in_=H.osum[:]), reads=["h_osum"], writes=[("dbg", 8)]))
            S.act(lambda e: e.activation(out=H.rs[:], in_=H.pA[:].rearrange("p c t -> p (c t)"), func=AF.Copy), reads=[("P", "pA")], writes=["h_rs"])
            outs.append(S.dma("sp", lambda e: e.dma_start(out=dbg[:, 9, :], in_=H.rs[:]), reads=["h_rs"], writes=[("dbg", 9)]))
            S.dve(lambda e: e.memset(H.Ek[:], 0.0), writes=["h_Ek"])
            S.dve(lambda e: e.tensor_copy(H.Ek[:, 0:4], H.lbv[:].rearrange("p a b -> p (a b)")), reads=["h_lbv"], writes=["h_Ek"])
            S.dve(lambda e: e.tensor_copy(H.Ek[:, 4:8], H.oml[:].rearrange("p a b -> p (a b)")), reads=["h_oml"], writes=["h_Ek"])
            S.dve(lambda e: e.tensor_copy(H.Ek[:, 8:12], H.den[:].rearrange("p a b -> p (a b)")), reads=["h_den"], writes=["h_Ek"])
            S.dve(lambda e: e.tensor_copy(H.Ek[:, 16:32], H.ehlb[:].rearrange("p a b c -> p (a b c)")), reads=["h_ehlb"], writes=["h_Ek"])
            outs.append(S.dma("sp", lambda e: e.dma_start(out=dbg[:, 11, :], in_=H.Ek[:]), reads=["h_Ek"], writes=[("dbg", 11)]))
            S.dve(lambda e: e.tensor_copy(H.Eq[:], H.At[:].rearrange("p c t -> p (c t)")), reads=["h_At"], writes=["h_Eq"])
            outs.append(S.dma("sp", lambda e: e.dma_start(out=dbg[:, 10, :], in_=H.Eq[:]), reads=["h_Eq"], writes=[("dbg", 10)]))
        if dr == 0:
            S.act(lambda e: e.activation(out=H.of[:, tok], in_=H.po[:], func=AF.Copy), reads=[("P", "po")], writes=["h_of"])
        else:
            S.dve(lambda e: e.tensor_tensor(H.osum[:], H.of[:, tok], H.po[:], ALU.add), reads=[("P", "po"), "h_of"], writes=["h_osum"])
            S.act(lambda e: e.activation(out=H.sq[:], in_=H.osum[:], func=AF.Square), reads=["h_osum"], writes=["h_sq"])
            S.pe(lambda e: e.matmul(H.pss[:], ones[:], H.sq[:], start=True, stop=True), reads=["h_sq", "ones"], writes=[("P", "pss")])
            S.act(lambda e: e.activation(out=H.rs[:], in_=H.pss[:], func=AF.Sqrt, bias=EPS, scale=1.0 / 128),
                  reads=[("P", "pss")], writes=["h_rs"])
            S.dve(lambda e: e.reciprocal(H.rs[:], H.rs[:]), reads=["h_rs"], writes=["h_rs"])
            S.dve(lambda e: e.tensor_tensor(H.osum[:], H.osum[:], H.rs[:], ALU.mult), reads=["h_osum", "h_rs"], writes=["h_osum"])
            S.dve(lambda e: e.scalar_tensor_tensor(out=H.obuf[:], in0=H.osum[:], scalar=H.gn[:, 0:1], in1=H.sgt[:, tok],
                                                   op0=ALU.mult, op1=ALU.mult),
                  reads=["h_osum", "h_gn", "h_sgt"], writes=["h_obuf"])
            g0 = b * seq + tt * 512
            o = S.dma("sp", lambda e: e.dma_start(out=o_out[hh, :, g0:g0 + 512], in_=H.obuf[:]),
                      reads=["h_obuf"], writes=[("o_out", hh, b, tt)])
            outs.append(o)

    load_h(0)
    for n, (hh, b, dr, tt) in enumerate(jobs):
        job(n, hh, b, dr, tt)
    return outs


def build_progB_hgrn(layer, nb=2, nh=2, nt=8, debug=False):
    nc = bass.Bass("TRN2", target_bir_lowering=False)
    hfull = nc.dram_tensor("hfull", [8, KC, 128, 1024], BF16, kind="ExternalInput").ap()
    wh = nc.dram_tensor("wh", [128, nh * 5 * 2048], F32, kind="ExternalInput").ap()
    hlb = nc.dram_tensor("hlb", [128, 2, 4, 2], F32, kind="ExternalInput").ap()
    gn = nc.dram_tensor("gn", [128, 1], F32, kind="ExternalInput").ap()
    cm = nc.dram_tensor("cm", [128, 2, 4, 128], I32, kind="ExternalInput").ap()
    rm = nc.dram_tensor("rm", [128, 512], F32, kind="ExternalInput").ap()
    ident = nc.dram_tensor("ident", [128, 128], F32, kind="ExternalInput").ap()
    o_out = nc.dram_tensor("o_out", [nh, 128, nb * nt * 512], BF16, kind="ExternalOutput").ap()
    S = Sched(nc)
    ones = make_consts(nc, S)
    H = HTiles(nc, nt)
    emit_hgrn_setup(S, H, layer, cm, rm, ident, hlb, gn)
    dbg = nc.dram_tensor("dbg", [128, 12, 512], F32, kind="ExternalOutput").ap() if debug else None
    outs = emit_hgrn(S, H, ones, nb, nh, nt, wh, hfull, o_out, dbg)
    S.emit(final_wait_ops=outs)
    return nc


DIL = (1, 4, 16)
ROPE_THETA = 500000.0


def attn_consts_host(seq=SEQ):
    pos = np.arange(seq, dtype=np.float32)
    inv_freq = (ROPE_THETA ** (-(np.arange(0, 32, 2, dtype=np.float32) / 32))).astype(np.float32)
    ang = pos[None, :] * inv_freq[:, None]
    rope = np.zeros((128, 2, seq), np.float32)
    rope[:, 0, :] = 1.0
    rope[0:16, 0] = np.cos(ang)
    rope[16:32, 0] = np.cos(ang)
    rope[0:16, 1] = np.sin(ang)
    rope[16:32, 1] = np.sin(ang)
    Rm = np.zeros((128, 128), np.float32)
    for m in range(16):
        Rm[m + 16, m] = -1.0
        Rm[m, m + 16] = 1.0
    a = np.arange(128)[:, None]
    b = np.arange(128)[None, :]
    am = np.zeros((128, 3, 128), np.float32)
    am[:, 0] = (a - b >= 64)
    am[:, 1] = (np.abs(a - b) <= 64)
    am[:, 2] = (b - a >= 64)
    return {"rope": rope, "Rm": Rm, "am": am, "ident": np.eye(128, dtype=np.float32)}


class ATiles:
    def __init__(self, nc, nt):
        seq = nt * 512
        a = nc.alloc_sbuf_tensor
        self.wq = a("a_wq", [128, 3, 2048], BF16)
        self.hTt = [a(f"a_hTt{i}", [128, KC, 512], BF16) for i in range(2)]
        self.rp = [a(f"a_rp{i}", [128, 2, 512], F32) for i in range(2)]
        self.acc = a("a_acc", [128, 2, seq], F32)
        self.QT = a("a_QT", [128, seq], BF16)
        self.KT = a("a_KT", [128, seq], BF16)
        self.VT = a("a_VT", [128, seq], BF16)
        self.Vs = a("a_Vs", [128, nt * 4, 128], BF16)
        self.qb = a("a_qb", [128, 512], BF16)
        self.t1 = a("a_t1", [128, 512], F32)
        self.t2 = a("a_t2", [128, 512], F32)
        self.PT = [a(f"a_PT{i}", [128, 3, 128], F32) for i in range(2)]
        self.PTm = [a(f"a_PTm{i}", [128, 3, 128], BF16) for i in range(2)]
        self.rec = a("a_rec", [128, 512], F32)
        self.obuf = a("a_obuf", [128, 512], BF16)
        self.Rf = a("a_Rf", [128, 128], F32)
        self.Rb = a("a_Rb", [128, 128], BF16)
        self.am = a("a_am", [128, 3, 128], F32)
        self.identf = a("a_identf", [128, 128], F32)
        self.ident = a("a_ident", [128, 128], BF16)
        p = nc.alloc_psum_tensor
        self.pj = [p(f"a_pj{i}", [128, 512], F32) for i in range(2)]
        self.prot = p("a_prot", [128, 512], F32)
        self.ptr = p("a_ptr", [128, 4, 128], BF16)
        self.pS = [p(f"a_pS{i}", [128, 3, 128], F32) for i in range(2)]
        self.pOL = [p(f"a_pOL{i}", [128, 2, 128], F32) for i in range(2)]


def emit_attn_setup(S, A, Rm_in, am_in, ident_in):
    S.dma("sp", lambda e: e.dma_start(out=A.Rf[:], in_=Rm_in), writes=["a_Rf"])
    S.dma("sp", lambda e: e.dma_start(out=A.am[:], in_=am_in), writes=["a_am"])
    S.dma("sp", lambda e: e.dma_start(out=A.identf[:], in_=ident_in), writes=["a_identf"])
    S.dve(lambda e: e.tensor_copy(A.Rb[:], A.Rf[:]), reads=["a_Rf"], writes=["a_Rb"])
    S.dve(lambda e: e.tensor_copy(A.ident[:], A.identf[:]), reads=["a_identf"], writes=["a_ident"])


def emit_attn(S, A, ones, nb, nh, nt, wa_dram, hfull, rope_dram, o_out):
    seq = nt * 512
    nblk = seq // 128
    outs = []
    scale = 128 ** -0.5
    jobs = [(hh, b, g, tt) for hh in range(nh) for b in range(nb) for g in range(3) for tt in range(nt)]

    def load_h(n):
        hh, b, g, tt = jobs[n]
        start = b * seq + tt * 512
        src, t0 = start // 1024, start % 1024
        buf = A.hTt[n % 2]
        rp = A.rp[n % 2]
        S.dma("sp", lambda e: e.dma_start(out=buf[:], in_=hfull[src, :, :, t0:t0 + 512].rearrange("k p t -> p k t")),
              writes=[("a_hTt", n % 2)])
        S.dma("sp", lambda e: e.dma_start(out=rp[:], in_=rope_dram[:, :, tt * 512:(tt + 1) * 512]),
              writes=[("a_rp", n % 2)])

    def proj(n, widx, ps, pskey):
        buf = A.hTt[n % 2]
        for kc in range(KC):
            S.pe(lambda e, kc=kc: e.matmul(ps[:], A.wq[:, widx, kc * 128:(kc + 1) * 128], buf[:, kc, :],
                                           start=(kc == 0), stop=(kc == KC - 1)),
                 reads=["a_wq", ("a_hTt", n % 2)], writes=[pskey])

    def proj_job(n, hh, b, g, tt):
        d = DIL[g]
        nu = 512 // d
        rp = A.rp[n % 2]
        if tt == 0:
            for w in range(3):
                c0 = ((hh * 3 + g) * 3 + w) * 2048
                S.dma("pool", lambda e, w=w, c0=c0: e.dma_start(out=A.wq[:, w, :], in_=wa_dram[:, c0:c0 + 2048]),
                      writes=["a_wq"])
        if n + 1 < len(jobs):
            load_h(n + 1)

        def dst(T_):
            return T_[:].rearrange("p (r u) -> p r u", r=d)[:, :, tt * nu:(tt + 1) * nu]

        def srcv(ap):
            return ap.rearrange("p (u r) -> p r u", r=d)

        for w, (T_, key) in enumerate(((A.QT, "a_QT"), (A.KT, "a_KT"))):
            ps = A.pj[w]
            pkey = ("P", "a_pj", w)
            proj(n, w, ps, pkey)
            S.act(lambda e, ps=ps: e.activation(out=A.qb[:], in_=ps[:], func=AF.Copy), reads=[pkey], writes=["a_qb"])
            S.pe(lambda e: e.matmul(A.prot[:], A.Rb[:], A.qb[:], start=True, stop=True),
                 reads=["a_Rb", "a_qb"], writes=[("P", "a_prot")])
            S.dve(lambda e, ps=ps: e.tensor_tensor(A.t1[:], ps[:], rp[:, 0, :], ALU.mult),
                  reads=[pkey, ("a_rp", n % 2)], writes=["a_t1"])
            S.dve(lambda e: e.tensor_tensor(A.t2[:], A.prot[:], rp[:, 1, :], ALU.mult),
                  reads=[("P", "a_prot"), ("a_rp", n % 2)], writes=["a_t2"])
            S.dve(lambda e, T_=T_: e.tensor_tensor(dst(T_), srcv(A.t1[:]), srcv(A.t2[:]), ALU.add),
                  reads=["a_t1", "a_t2"], writes=[key])
        proj(n, 2, A.pj[0], ("P", "a_pj", 0))
        S.act(lambda e: e.activation(out=dst(A.VT), in_=srcv(A.pj[0][:]), func=AF.Copy),
              reads=[("P", "a_pj", 0)], writes=["a_VT"])

    def attn_pass(hh, b, g):
        d = DIL[g]
        L = seq // d
        nI = L // 128
        for q4 in range(nblk // 4):
            for i in range(4):
                blk = q4 * 4 + i
                S.pe(lambda e, i=i, blk=blk: e.transpose(A.ptr[:, i, :], A.VT[:, blk * 128:(blk + 1) * 128], A.ident[:]),
                     reads=["a_VT", "a_ident"], writes=[("P", "a_ptr")])
            S.dve(lambda e, q4=q4: e.tensor_copy(A.Vs[:, q4 * 4:q4 * 4 + 4, :], A.ptr[:]),
                  reads=[("P", "a_ptr")], writes=["a_Vs"])
        accv = A.acc[:].rearrange("p a (u r) -> p a u r", r=d)
        it = 0
        for r in range(d):
            for I in range(nI):
                qblk = r * nI + I
                js = [j for j in range(3) if 0 <= I + j - 1 < nI]
                j0, j1 = js[0], js[-1] + 1
                par = it % 2
                it += 1
                pS, PT, PTm, pOL = A.pS[par], A.PT[par], A.PTm[par], A.pOL[par]
                for j in js:
                    kblk = r * nI + I + j - 1
                    S.pe(lambda e, j=j, kblk=kblk, qblk=qblk, pS=pS: e.matmul(
                        pS[:, j, :], A.KT[:, kblk * 128:(kblk + 1) * 128], A.QT[:, qblk * 128:(qblk + 1) * 128],
                        start=True, stop=True),
                        reads=["a_KT", "a_QT"], writes=[("P", "a_pS", par)])
                S.act(lambda e, pS=pS, PT=PT, j0=j0, j1=j1: e.activation(out=PT[:, j0:j1, :], in_=pS[:, j0:j1, :], func=AF.Exp, scale=scale),
                      reads=[("P", "a_pS", par)], writes=[("a_PT", par)])
                S.dve(lambda e, PT=PT, PTm=PTm, j0=j0, j1=j1: e.tensor_tensor(PTm[:, j0:j1, :], PT[:, j0:j1, :], A.am[:, j0:j1, :], ALU.mult),
                      reads=[("a_PT", par), "a_am"], writes=[("a_PTm", par)])
                for ji, j in enumerate(js):
                    kblk = r * nI + I + j - 1
                    S.pe(lambda e, j=j, kblk=kblk, ji=ji, pOL=pOL, PTm=PTm, n_=len(js): e.matmul(
                        pOL[:, 0, :], A.Vs[:, kblk, :], PTm[:, j, :], start=(ji == 0), stop=(ji == n_ - 1)),
                        reads=["a_Vs", ("a_PTm", par)], writes=[("P", "a_pOL", par)])
                for ji, j in enumerate(js):
                    S.pe(lambda e, j=j, ji=ji, pOL=pOL, PTm=PTm, n_=len(js): e.matmul(
                        pOL[:, 1, :], ones[:], PTm[:, j, :], start=(ji == 0), stop=(ji == n_ - 1)),
                        reads=["ones", ("a_PTm", par)], writes=[("P", "a_pOL", par)])
                dstv = accv[:, :, I * 128:(I + 1) * 128, r]
                if g == 0:
                    S.dve(lambda e, pOL=pOL, dstv=dstv: e.tensor_copy(dstv, pOL[:]),
                          reads=[("P", "a_pOL", par)], writes=["a_acc"])
                else:
                    S.dve(lambda e, pOL=pOL, dstv=dstv: e.tensor_tensor(dstv, dstv, pOL[:], ALU.add),
                          reads=[("P", "a_pOL", par), "a_acc"], writes=["a_acc"])

    def finalize(hh, b):
        for tt in range(nt):
            tok = slice(tt * 512, (tt + 1) * 512)
            S.dve(lambda e, tok=tok: e.reciprocal(A.rec[:], A.acc[:, 1, tok]), reads=["a_acc"], writes=["a_rec"])
            S.dve(lambda e, tok=tok: e.tensor_tensor(A.obuf[:], A.acc[:, 0, tok], A.rec[:], ALU.mult),
                  reads=["a_acc", "a_rec"], writes=["a_obuf"])
            g0 = b * seq + tt * 512
            o = S.dma("sp", lambda e, g0=g0: e.dma_start(out=o_out[hh, :, g0:g0 + 512], in_=A.obuf[:]),
                      reads=["a_obuf"], writes=[("o_out", hh, b, tt)])
            outs.append(o)

    load_h(0)
    for n, (hh, b, g, tt) in enumerate(jobs):
        proj_job(n, hh, b, g, tt)
        if tt == nt - 1:
            attn_pass(hh, b, g)
            if g == 2:
                finalize(hh, b)
    return outs


def build_progB_attn(nb=2, nh=2, nt=8):
    nc = bass.Bass("TRN2", target_bir_lowering=False)
    seq = nt * 512
    hfull = nc.dram_tensor("hfull", [8, KC, 128, 1024], BF16, kind="ExternalInput").ap()
    wa = nc.dram_tensor("wa", [128, nh * 9 * 2048], F32, kind="ExternalInput").ap()
    rope = nc.dram_tensor("rope", [128, 2, seq], F32, kind="ExternalInput").ap()
    Rm = nc.dram_tensor("Rm", [128, 128], F32, kind="ExternalInput").ap()
    am = nc.dram_tensor("am", [128, 3, 128], F32, kind="ExternalInput").ap()
    ident = nc.dram_tensor("ident", [128, 128], F32, kind="ExternalInput").ap()
    o_out = nc.dram_tensor("o_out", [nh, 128, nb * seq], BF16, kind="ExternalOutput").ap()
    S = Sched(nc)
    ones = make_consts(nc, S)
    A = ATiles(nc, nt)
    emit_attn_setup(S, A, Rm, am, ident)
    outs = emit_attn(S, A, ones, nb, nh, nt, wa, hfull, rope, o_out)
    S.emit(final_wait_ops=outs)
    return nc


def _fm(a):
    return np.ascontiguousarray(a.T).reshape(KC, 128, a.shape[0])


def _hgrn_core_inputs(w_in_hgrn_slot, hgrn_lower_bounds, hgrn_gnorm_slot, c):
    tiles = []
    for hh in range(2):
        head = 2 * c + hh
        for blk in range(5):
            tiles.append(w_tile(w_in_hgrn_slot, blk * D + head * 128))
    wh = np.concatenate(tiles, axis=1)
    hlb = np.empty((128, 2, 4, 2), np.float32)
    for hh in range(2):
        head = 2 * c + hh
        hlb[:, :, :, hh] = hgrn_lower_bounds[:, :, head * 128:(head + 1) * 128].transpose(2, 0, 1)
    gn = np.ascontiguousarray(hgrn_gnorm_slot.reshape(128, 1))
    return wh, hlb, gn


def _attn_core_inputs(w_in_attn_slot, c):
    tiles = []
    for hh in range(2):
        head = 2 * c + hh
        for g in range(3):
            for w in range(3):
                tiles.append(w_tile(w_in_attn_slot, ((g * 3 + w) * 16 + head) * 128))
    return np.concatenate(tiles, axis=1)


def _o_exchange(o_outs):
    res = []
    for j in range(NCORES):
        t = np.empty((KC, 128, TPC), o_outs[0].dtype)
        for c in range(NCORES):
            t[2 * c:2 * c + 2] = o_outs[c][:, :, j * TPC:(j + 1) * TPC]
        res.append(t)
    return res


def kernel(x, norm_w, w_in_hgrn, hgrn_lower_bounds, hgrn_gnorm, w_in_attn, w_out, w_ffn_in, w_ffn_out):
    x = np.asarray(x, np.float32)
    norm_w = np.asarray(norm_w, np.float32)
    w_in_hgrn = np.asarray(w_in_hgrn, np.float32)
    hgrn_lower_bounds = np.asarray(hgrn_lower_bounds, np.float32)
    hgrn_gnorm = np.asarray(hgrn_gnorm, np.float32)
    w_in_attn = np.asarray(w_in_attn, np.float32)
    w_out = np.asarray(w_out, np.float32)
    w_ffn_in = np.asarray(w_ffn_in, np.float32)
    w_ffn_out = np.asarray(w_ffn_out, np.float32)
    cores = list(range(NCORES))
    xf = x.reshape(NTOK, D)
    xs = [_fm(xf[c * TPC:(c + 1) * TPC]) for c in cores]
    nw_in = np.ascontiguousarray(norm_w.reshape(DEPTH, 4, KC, 128).transpose(3, 0, 1, 2))
    hc = hgrn_consts_host()
    ac = attn_consts_host()

    res = run_bass_kernel_spmd(build_progA(), [{"x_in": xs[c], "nw_in": nw_in} for c in cores], core_ids=cores)
    hfull = np.stack([res.results[c]["h_out"] for c in cores], axis=0)
    for l in range(DEPTH):
        slot = l // 2
        if l % 2 == 0:
            in_maps = []
            for c in cores:
                wh, hlb, gn = _hgrn_core_inputs(w_in_hgrn[slot], hgrn_lower_bounds, hgrn_gnorm[slot], c)
                in_maps.append({"hfull": hfull, "wh": wh, "hlb": hlb, "gn": gn,
                                "cm": hc["cm"], "rm": hc["rm"], "ident": hc["ident"]})
            res = run_bass_kernel_spmd(build_progB_hgrn(l), in_maps, core_ids=cores)
        else:
            in_maps = []
            for c in cores:
                in_maps.append({"hfull": hfull, "wa": _attn_core_inputs(w_in_attn[slot], c), "rope": ac["rope"],
                                "Rm": ac["Rm"], "am": ac["am"], "ident": ac["ident"]})
            res = run_bass_kernel_spmd(build_progB_attn(), in_maps, core_ids=cores)
        o_ins = _o_exchange([res.results[c]["o_out"] for c in cores])
        wc = pack_phaseC_weights(w_out[l], w_ffn_in[l], w_ffn_out[l])
        res = run_bass_kernel_spmd(build_progC(l),
                                   [{"x_in": xs[c], "o_in": o_ins[c], "nw_in": nw_in, "wc": wc} for c in cores],
                                   core_ids=cores)
        xs = [res.results[c]["x_out"] for c in cores]
        if l < DEPTH - 1:
            hfull = np.stack([res.results[c]["h_out"] for c in cores], axis=0)
    out = np.concatenate([xs[c].reshape(D, TPC).T for c in cores], axis=0)
    return np.ascontiguousarray(out.reshape(2, SEQ, D).astype(np.float32))
```
